# Optimizing a Trainium2 kernel written in Bass

```python
import jax, jax.numpy as jnp
from jax import lax
import numpy as np

D_MODEL = 2048
BATCH = 1
SEQ = 8192
DEPTH = 4

HEAD_DIM = 128
NORM_EPS = 1e-6
SB_HEADS = 8
SB_WIDTH = SB_HEADS * HEAD_DIM
SB_BLOCK = 128
CONV_WIDTH = 1024
CONV_KERNEL = 31
SGU_WIDTH = 1024
SGU_GROUPS = 8
SGU_CHUNK = 128
DIL_PATTERNS = ((128, 1), (512, 4), (2048, 16))
DIL_NGROUPS = 3
DIL_SLOTS = 8
DIL_WIDTH = DIL_SLOTS * HEAD_DIM
DIL_BLOCK = 128
N_BRANCH = 4
BRANCH_WIDTH = 1024
IN_SIZES = (SB_WIDTH, SB_WIDTH, SB_WIDTH, SB_WIDTH,
            CONV_WIDTH, CONV_WIDTH, CONV_WIDTH,
            SGU_WIDTH, SGU_WIDTH, SGU_WIDTH,
            DIL_NGROUPS * DIL_WIDTH, DIL_NGROUPS * DIL_WIDTH,
            DIL_WIDTH, DIL_WIDTH)
IN_WIDTH = 18432

kernel_name = "hybrid_sb_conv_sgu_dilated_block"


def rmsnorm(x, g):
    xf = x.astype(jnp.float32)
    y = xf * lax.rsqrt(jnp.mean(xf * xf, axis=-1, keepdims=True) + NORM_EPS)
    return (y * g.astype(jnp.float32)).astype(x.dtype)


def layernorm(x, g, b):
    xf = x.astype(jnp.float32)
    mu = jnp.mean(xf, axis=-1, keepdims=True)
    var = jnp.mean(jnp.square(xf - mu), axis=-1, keepdims=True)
    y = (xf - mu) * lax.rsqrt(var + NORM_EPS)
    return (y * g.astype(jnp.float32) + b.astype(jnp.float32)).astype(x.dtype)


def stick_breaking_attention(q, k, v):
    B, S, H, Dh = q.shape
    nb = S // SB_BLOCK
    scale = Dh ** -0.5
    qb = q.astype(jnp.float32).reshape(B, nb, SB_BLOCK, H, Dh).transpose(1, 0, 3, 2, 4)
    kf = k.astype(jnp.float32)
    vf = v.astype(jnp.float32)
    key_pos = jnp.arange(S)

    def block(args):
        qi, n = args
        z = jnp.einsum('bhqd,bkhd->bhqk', qi, kf) * scale
        q_pos = n * SB_BLOCK + jnp.arange(SB_BLOCK)
        causal = key_pos[None, :] < q_pos[:, None]
        log_beta = jax.nn.log_sigmoid(z)
        log_1mb = jnp.where(causal, jax.nn.log_sigmoid(-z), 0.0)
        after = lax.cumsum(log_1mb, axis=3, reverse=True) - log_1mb
        w = jnp.where(causal, jnp.exp(log_beta + after), 0.0)
        return jnp.einsum('bhqk,bkhd->bqhd', w, vf)

    out = lax.map(block, (qb, jnp.arange(nb)))
    return out.transpose(1, 0, 2, 3, 4).reshape(B, S, H * Dh)


def causal_depthwise_conv(x, w, b):
    K, C = w.shape
    xp = jnp.pad(x, ((0, 0), (K - 1, 0), (0, 0)))
    y = lax.conv_general_dilated(xp, w[:, None, :], window_strides=(1,), padding='VALID',
                                 dimension_numbers=('NWC', 'WIO', 'NWC'), feature_group_count=C)
    return y + b


def conformer_conv(glu_a, glu_b, conv_w, conv_b, ln_g, ln_b):
    g = glu_a * jax.nn.sigmoid(glu_b)
    c = causal_depthwise_conv(g, conv_w, conv_b)
    return jax.nn.silu(layernorm(c, ln_g, ln_b))


def spatial_gating(u, v, w_s, b_s, ln_g, ln_b):
    B, S, C = u.shape
    n = S // SGU_CHUNK
    G = SGU_GROUPS
    vn = layernorm(v, ln_g, ln_b).reshape(B, n, SGU_CHUNK, G, C // G)
    mask = jnp.tril(jnp.ones((SGU_CHUNK, SGU_CHUNK), dtype=bool))
    w = jnp.where(mask[None], w_s, jnp.zeros_like(w_s))
    z = jnp.einsum('gts,bnsgc->bntgc', w, vn) + b_s.T[None, None, :, :, None]
    return u * z.reshape(B, S, C)


def dilated_group_attention(q, k, v, window, dil):
    B, S, H, Dh = q.shape
    n_keys = window // dil
    L = S // dil
    N = B * dil
    T = DIL_BLOCK
    nb = -(-L // T)
    Lp = nb * T

    def to_classes(t):
        t = t.astype(jnp.float32).reshape(B, L, dil, H, Dh).transpose(0, 2, 1, 3, 4).reshape(N, L, H, Dh)
        t = jnp.pad(t, ((0, 0), (0, Lp - L), (0, 0), (0, 0)))
        return t.reshape(N, nb, T, H, Dh)

    def with_prev(t):
        prev = jnp.pad(t[:, :-1], ((0, 0), (1, 0), (0, 0), (0, 0), (0, 0)))
        return jnp.concatenate([prev, t], axis=2)

    qb = to_classes(q)
    kw = with_prev(to_classes(k))
    vw = with_prev(to_classes(v))
    s = jnp.einsum('nbqhd,nbkhd->nbhqk', qb, kw) * (Dh ** -0.5)
    a = jnp.arange(T)[:, None]
    c = jnp.arange(2 * T)[None, :]
    rel = T + a - c
    blk = jnp.arange(nb)[:, None, None]
    valid = (rel >= 0) & (rel <= n_keys) & ((blk > 0) | (c >= T))
    s = jnp.where(valid[None, :, None], s, -jnp.inf)
    m = jnp.max(s, axis=-1, keepdims=True)
    p = jnp.exp(s - m)
    den = jnp.sum(p, axis=-1, keepdims=True)
    o = jnp.einsum('nbhqk,nbkhd->nbqhd', p, vw) / den.transpose(0, 1, 3, 2, 4)
    lse = (m + jnp.log(den))[..., 0].transpose(0, 1, 3, 2)

    def from_classes(t):
        tail = t.shape[3:]
        t = t.reshape((N, Lp) + tail)[:, :L]
        t = jnp.moveaxis(t.reshape((B, dil, L) + tail), 1, 2)
        return t.reshape((B, S) + tail)

    return from_classes(o), from_classes(lse)


def dilated_mixture(q, k, v):
    outs, lses = [], []
    for g, (window, dil) in enumerate(DIL_PATTERNS):
        o, lse = dilated_group_attention(q[:, :, g], k[:, :, g], v, window, dil)
        outs.append(o)
        lses.append(lse)
    w = jax.nn.softmax(jnp.stack(lses, 0), axis=0)
    return jnp.einsum('gbsh,gbshd->bshd', w, jnp.stack(outs, 0))


def hybrid_layer(x, norm_g, w_in, conv_w, conv_b, conv_ln_g, conv_ln_b, sgu_ln_g, sgu_ln_b,
                 sgu_w, sgu_b, w_branch, w_gate, b_gate, w_out):
    B, S, D = x.shape
    h = rmsnorm(x, norm_g)
    proj = jnp.einsum('bsd,df->bsf', h, w_in)
    splits = [int(i) for i in np.cumsum(IN_SIZES)[:-1]]
    (a_q, a_k, a_v, a_g, b_a, b_b, b_g, c_u, c_v, c_g, d_q, d_k, d_v, d_g) = jnp.split(proj, splits, axis=-1)

    shp = (B, S, SB_HEADS, HEAD_DIM)
    ya = stick_breaking_attention(a_q.reshape(shp), a_k.reshape(shp), a_v.reshape(shp)).astype(x.dtype)
    ya = ya * jax.nn.silu(a_g)
    yb = conformer_conv(b_a, b_b, conv_w, conv_b, conv_ln_g, conv_ln_b) * jax.nn.silu(b_g)
    yc = spatial_gating(jax.nn.gelu(c_u), jax.nn.gelu(c_v), sgu_w, sgu_b, sgu_ln_g, sgu_ln_b) * jax.nn.silu(c_g)
    gshp = (B, S, DIL_NGROUPS, DIL_SLOTS, HEAD_DIM)
    yd = dilated_mixture(d_q.reshape(gshp), d_k.reshape(gshp), d_v.reshape(B, S, DIL_SLOTS, HEAD_DIM))
    yd = yd.reshape(B, S, DIL_WIDTH).astype(x.dtype) * jax.nn.silu(d_g)

    branches = jnp.stack([ya, yb, yc, yd], axis=2)
    yproj = jnp.einsum('bsnc,ncd->bsnd', branches, w_branch)
    gates = jax.nn.sigmoid(jnp.einsum('bsd,df->bsf', h, w_gate) + b_gate).reshape(B, S, N_BRANCH, D)
    merged = jnp.sum(gates * yproj, axis=2)
    return x + jnp.einsum('bsd,de->bse', merged, w_out)


def setup_inputs(seed: int = 0) -> dict:
    key = jax.random.key(seed)
    ks = jax.random.split(key, 17)
    f32 = jnp.float32
    nrm = lambda k, shape, scale: jax.random.normal(k, shape, f32) * scale
    return {
        "x": nrm(ks[0], (BATCH, SEQ, D_MODEL), 1.0),
        "norm_g": 1.0 + nrm(ks[1], (DEPTH, D_MODEL), 0.01),
        "w_in": nrm(ks[2], (DEPTH, D_MODEL, IN_WIDTH), D_MODEL ** -0.5),
        "conv_w": nrm(ks[3], (DEPTH, CONV_KERNEL, CONV_WIDTH), CONV_KERNEL ** -0.5),
        "conv_b": nrm(ks[4], (DEPTH, CONV_WIDTH), 0.01),
        "conv_ln_g": 1.0 + nrm(ks[5], (DEPTH, CONV_WIDTH), 0.01),
        "conv_ln_b": nrm(ks[6], (DEPTH, CONV_WIDTH), 0.01),
        "sgu_ln_g": 1.0 + nrm(ks[7], (DEPTH, SGU_WIDTH), 0.01),
        "sgu_ln_b": nrm(ks[8], (DEPTH, SGU_WIDTH), 0.01),
        "sgu_w": nrm(ks[9], (DEPTH, SGU_GROUPS, SGU_CHUNK, SGU_CHUNK), SGU_CHUNK ** -0.5),
        "sgu_b": 1.0 + nrm(ks[10], (DEPTH, SGU_GROUPS, SGU_CHUNK), 0.01),
        "w_branch": nrm(ks[11], (DEPTH, N_BRANCH, BRANCH_WIDTH, D_MODEL), BRANCH_WIDTH ** -0.5),
        "w_gate": nrm(ks[12], (DEPTH, D_MODEL, N_BRANCH * D_MODEL), D_MODEL ** -0.5),
        "b_gate": nrm(ks[13], (DEPTH, N_BRANCH * D_MODEL), 0.01),
        "w_out": nrm(ks[14], (DEPTH, D_MODEL, D_MODEL), D_MODEL ** -0.5),
        "final_g": 1.0 + nrm(ks[15], (D_MODEL,), 0.01),
    }


def reference(x, norm_g, w_in, conv_w, conv_b, conv_ln_g, conv_ln_b, sgu_ln_g, sgu_ln_b,
              sgu_w, sgu_b, w_branch, w_gate, b_gate, w_out, final_g):
    for l in range(DEPTH):
        x = hybrid_layer(x, norm_g[l], w_in[l], conv_w[l], conv_b[l], conv_ln_g[l], conv_ln_b[l],
                         sgu_ln_g[l], sgu_ln_b[l], sgu_w[l], sgu_b[l], w_branch[l], w_gate[l],
                         b_gate[l], w_out[l])
    return rmsnorm(x, final_g)
```

```python
import contextlib
import numpy as np
import ml_dtypes
import concourse.bass as bass
import concourse.mybir as mybir
from concourse.bass_utils import run_bass_kernel_spmd

F32 = mybir.dt.float32
BF16 = mybir.dt.bfloat16
AF = mybir.ActivationFunctionType
ALU = mybir.AluOpType
AX = mybir.AxisListType

D = 2048
S = 8192
NCORE = 8
TOK = S // NCORE
TT = 512
DEPTH = 4
EPS = 1e-6
IN_W = 18432
SCALE = 128 ** -0.5


class Buf:
    _n = 0

    def __init__(self, name):
        Buf._n += 1
        self.name = f"{name}#{Buf._n}"
        self.state = {}


class Acc:
    __slots__ = ("ap", "buf", "keys")

    def __init__(self, ap, buf, keys):
        self.ap = ap
        self.buf = buf
        self.keys = keys


class _View:
    def __init__(self, tile, keys):
        self.tile = tile
        self.keys = keys

    def __getitem__(self, idx):
        return Acc(self.tile.t[idx], self.tile.buf, self.keys)


class Tile:
    def __init__(self, t, name, track=True):
        self.t = t
        self.buf = Buf(name) if track else None

    def __getitem__(self, idx):
        return Acc(self.t[idx], self.buf, (None,))

    def k(self, *keys):
        return _View(self, keys)


class Sub:
    def __init__(self, tile, c0, width, key):
        self.tile, self.c0, self.width, self.key = tile, c0, width, key

    def __getitem__(self, idx):
        p, c = idx
        a = 0 if c.start is None else c.start
        b = self.width if c.stop is None else c.stop
        return Acc(self.tile.t[p, self.c0 + a:self.c0 + b], self.tile.buf, (self.key,))


class Op:
    __slots__ = ("eng", "fn", "deps", "signal", "rank", "is_dma", "dsem", "dval", "semi", "acc", "inc")

    def __init__(self, eng, fn, is_dma):
        self.eng = eng
        self.fn = fn
        self.deps = set()
        self.signal = False
        self.rank = None
        self.is_dma = is_dma
        self.dsem = None
        self.dval = None
        self.semi = 0
        self.acc = False
        self.inc = 16


ENGS = ("pe", "act", "dve", "pool", "sp")
SEM_LIMIT = 30000


class Prog:
    def __init__(self, nc, st):
        self.nc = nc
        self.st = st
        self.streams = {e: [] for e in ENGS}
        self.dma_slots = {}
        self.nops = 0

    def sb(self, name, shape, dt):
        return Tile(self.st.enter_context(self.nc.sbuf_tensor(name, list(shape), dt)), name)

    def ps(self, name, shape=(128, 512), dt=F32):
        return Tile(self.st.enter_context(self.nc.psum_tensor(name, list(shape), dt)), name)

    def dram(self, name, shape, dt, kind="Internal", track=True):
        if kind == "Internal":
            h = self.nc.dram_tensor(name, list(shape), dt)
        else:
            h = self.nc.dram_tensor(name, list(shape), dt, kind=kind)
        return Tile(h.ap(), name, track=track)

    @staticmethod
    def _conf(buf, key):
        if key is None:
            return list(buf.state.keys())
        ks = []
        if key in buf.state:
            ks.append(key)
        if None in buf.state:
            ks.append(None)
        return ks

    def op(self, eng, fn, reads=(), writes=(), is_dma=False, accw=(), inc=16):
        o = Op(eng, fn, is_dma)
        o.inc = inc
        rd = [(a.buf, k) for a in reads if isinstance(a, Acc) and a.buf is not None for k in a.keys]
        wr = [(a.buf, k) for a in writes if isinstance(a, Acc) and a.buf is not None for k in a.keys]
        aw = [(a.buf, k) for a in accw if isinstance(a, Acc) and a.buf is not None for k in a.keys]
        o.acc = bool(aw)
        for (b, k) in rd:
            for kk in self._conf(b, k):
                o.deps.update(b.state[kk][0])
        for (b, k) in wr:
            for kk in self._conf(b, k):
                st = b.state[kk]
                o.deps.update(st[0])
                for r in st[1]:
                    if r.eng == eng and not r.is_dma and not is_dma:
                        continue
                    o.deps.add(r)
        for (b, k) in aw:
            for kk in self._conf(b, k):
                st = b.state[kk]
                o.deps.update(st[1])
                for w in st[0]:
                    if not w.acc:
                        o.deps.add(w)
        o.deps.discard(o)
        for (b, k) in rd:
            b.state.setdefault(k, [[], []])[1].append(o)
        for (b, k) in wr:
            if k is None:
                b.state = {None: [[o], []]}
            else:
                b.state[k] = [[o], []]
        for (b, k) in aw:
            st = b.state.setdefault(k, [[], []])
            if st[1] or any(not w.acc for w in st[0]):
                st[0] = []
                st[1] = []
            st[0].append(o)
        if is_dma:
            tgt = (wr + aw)[0]
            slot = (tgt[0].name, tgt[1], rd[0][0].name if (aw and rd) else None)
            ent = self.dma_slots.setdefault(slot, [0, 0])
            ent[1] += inc
            o.dsem = slot
            o.dval = ent[1]
        self.streams[eng].append(o)
        self.nops += 1
        return o

    @staticmethod
    def _v(x):
        return x.ap if isinstance(x, Acc) else x

    def mm(self, out, lhsT, rhs, start=True, stop=True):
        return self.op("pe", lambda e: e.matmul(out.ap, lhsT=lhsT.ap, rhs=rhs.ap, start=start, stop=stop),
                       reads=[lhsT, rhs], writes=[out])

    def tr(self, out, in_, ident):
        return self.op("pe", lambda e: e.transpose(out.ap, in_.ap, ident.ap), reads=[in_, ident], writes=[out])

    def act(self, out, in_, func, bias=None, scale=None, accum=None, eng="act"):
        kw = {}
        if bias is not None:
            kw["bias"] = self._v(bias)
        if scale is not None:
            kw["scale"] = self._v(scale)
        if accum is not None:
            kw["accum_out"] = accum.ap
        wr = [out] + ([accum] if accum is not None else [])
        return self.op(eng, lambda e: e.activation(out=out.ap, in_=in_.ap, func=func, **kw),
                       reads=[in_, bias, scale], writes=wr)

    def tt(self, eng, out, a, b, op):
        return self.op(eng, lambda e: e.tensor_tensor(out=out.ap, in0=a.ap, in1=b.ap, op=op), reads=[a, b], writes=[out])

    def ts(self, eng, out, a, s1, s2, op0, op1=None):
        if op1 is None:
            return self.op(eng, lambda e: e.tensor_scalar(out=out.ap, in0=a.ap, scalar1=self._v(s1), scalar2=None, op0=op0),
                           reads=[a, s1], writes=[out])
        return self.op(eng, lambda e: e.tensor_scalar(out=out.ap, in0=a.ap, scalar1=self._v(s1), scalar2=self._v(s2), op0=op0, op1=op1),
                       reads=[a, s1, s2], writes=[out])

    def stt(self, eng, out, a, s, b, op0, op1):
        return self.op(eng, lambda e: e.scalar_tensor_tensor(out=out.ap, in0=a.ap, scalar=self._v(s), in1=b.ap, op0=op0, op1=op1),
                       reads=[a, s, b], writes=[out])

    def cp(self, eng, out, a):
        if eng == "act":
            return self.act(out, a, AF.Copy)
        return self.op(eng, lambda e: e.tensor_copy(out=out.ap, in_=a.ap), reads=[a], writes=[out])

    def memset(self, eng, out, val):
        return self.op(eng, lambda e: e.memset(out.ap, val), writes=[out])

    def dma(self, eng, out, in_, acc=False):
        if acc:
            return self.op(eng, lambda e: e.dma_start(out=out.ap, in_=in_.ap), reads=[in_], accw=[out], is_dma=True)
        return self.op(eng, lambda e: e.dma_start(out=out.ap, in_=in_.ap), reads=[in_], writes=[out], is_dma=True)

    def emit(self, final_waits):
        nc = self.nc
        for e in ENGS:
            for o in self.streams[e]:
                for d in o.deps:
                    if d.is_dma or (d.eng == o.eng == "pe"):
                        continue
                    d.signal = True
        nsem = {}
        for e in ENGS:
            r = 0
            si = 0
            for o in self.streams[e]:
                if o.is_dma or not o.signal:
                    continue
                r += 1
                if r > SEM_LIMIT:
                    si += 1
                    r = 1
                o.rank = r
                o.semi = si
            nsem[e] = si + 1
        st = self.st
        esems = {e: [st.enter_context(nc.semaphore(f"s_{e}{i}")) for i in range(nsem[e])] for e in ENGS}
        dsems = {slot: st.enter_context(nc.semaphore(f"d{i}")) for i, slot in enumerate(self.dma_slots)}
        self.n_sems = sum(nsem.values()) + len(dsems)
        block = st.enter_context(nc.Block())

        def run(ename):
            def body(eng):
                waited = {}
                for o in self.streams[ename]:
                    need = {}
                    for d in o.deps:
                        if d.is_dma:
                            key, val = ("d", d.dsem), d.dval
                        else:
                            if d.eng == ename == "pe":
                                continue
                            key, val = ("e", d.eng, d.semi), d.rank
                        if need.get(key, 0) < val:
                            need[key] = val
                    for key, val in need.items():
                        if waited.get(key, 0) >= val:
                            continue
                        waited[key] = val
                        sem = dsems[key[1]] if key[0] == "d" else esems[key[1]][key[2]]
                        eng.wait_ge(sem, val)
                    ins = o.fn(eng)
                    if o.is_dma:
                        if o.inc == 16:
                            ins.then_inc(dsems[o.dsem], 16)
                        else:
                            ins.then_inc(dsems[o.dsem])
                    elif o.signal:
                        ins.then_inc(esems[ename][o.semi], 1)
                for t in final_waits.get(ename, ()):
                    for slot in self.dma_slots:
                        if slot[0] == t.buf.name:
                            eng.wait_ge(dsems[slot], self.dma_slots[slot][1])
            return body

        block.tensor(run("pe"))
        block.scalar(run("act"))
        block.vector(run("dve"))
        block.gpsimd(run("pool"))
        block.sync(run("sp"))


class Ctx:
    def __init__(self, P):
        self.P = P
        self.psum = [P.ps(f"ps{i}") for i in range(7)]
        self.ptb = P.ps("ptb", (128, 512), BF16)
        self.pi = 0
        self.ones_f = P.sb("ones_f", [128, 128], F32)
        self.ones_b = P.sb("ones_b", [128, 128], BF16)
        self.eps_col = P.sb("eps_col", [128, 1], F32)
        P.memset("pool", self.ones_f[:], 1.0)
        P.memset("pool", self.ones_b[:], 1.0)
        P.memset("pool", self.eps_col[:], EPS)

    def bank(self):
        b = self.psum[self.pi % 7]
        self.pi += 1
        return b


def emit_rmsnorm(P, C, xt, g_cols, out_tile, n_tok, tmps):
    ssq = C.bank()
    for c in range(16):
        sq = tmps["sq"][c % 2]
        hi = tmps["hi"][c % 2]
        lo = tmps["lo"][c % 2]
        P.act(sq[:, 0:n_tok], xt.k(c)[:, c, 0:n_tok], AF.Square)
        P.cp("dve", hi[:, 0:n_tok], sq[:, 0:n_tok])
        P.tt("dve", lo[:, 0:n_tok], sq[:, 0:n_tok], hi[:, 0:n_tok], ALU.subtract)
        P.mm(ssq[:, 0:n_tok], C.ones_b[:], hi[:, 0:n_tok], start=(c == 0), stop=False)
        P.mm(ssq[:, 0:n_tok], C.ones_b[:], lo[:, 0:n_tok], start=False, stop=(c == 15))
    stat = tmps["stat"]
    P.act(stat[:, 0:n_tok], ssq[:, 0:n_tok], AF.Ln, bias=C.eps_col[:, 0:1], scale=1.0 / D)
    P.act(stat[:, 0:n_tok], stat[:, 0:n_tok], AF.Exp, scale=-0.5)
    for c in range(16):
        P.stt("dve", out_tile.k(c)[:, c, 0:n_tok], xt.k(c)[:, c, 0:n_tok], g_cols[:, c:c + 1], stat[:, 0:n_tok],
              ALU.mult, ALU.mult)


def fm(ap):
    return ap.rearrange("(c p) t -> p c t", p=128)


def U(ap):
    return Acc(ap, None, (None,))


OFF_AQ, OFF_AK, OFF_AV, OFF_AG = 0, 1024, 2048, 3072
OFF_BA, OFF_BB, OFF_BG = 4096, 5120, 6144
OFF_CU, OFF_CV, OFF_CG = 7168, 8192, 9216
OFF_DQ, OFF_DK, OFF_DV, OFF_DG = 10240, 13312, 16384, 17408
NVEC = 104


def build_p1():
    nc = bass.Bass("TRN2", target_bir_lowering=False)
    with contextlib.ExitStack() as st:
        P = Prog(nc, st)
        xT = P.dram("xT", [D, TOK], F32, kind="ExternalInput", track=False)
        vecs = P.dram("vecs", [128, NVEC], F32, kind="ExternalInput", track=False)
        ho = P.dram("ho", [D, TOK], BF16, kind="ExternalOutput")
        C = Ctx(P)
        vec_sb = P.sb("vec_sb", [128, NVEC], F32)
        P.dma("sp", vec_sb[:], U(vecs.t[:, :]))
        R = P.sb("R", [128, 16, TT], F32)
        hn = P.sb("hn", [128, 16, TT], BF16)
        tmps = {"sq": [P.sb(f"sq{i}", [128, TT], F32) for i in range(2)],
                "hi": [P.sb(f"hi{i}", [128, TT], BF16) for i in range(2)],
                "lo": [P.sb(f"lo{i}", [128, TT], BF16) for i in range(2)],
                "stat": P.sb("stat", [128, TT], F32)}
        for ti in range(TOK // TT):
            t0 = ti * TT
            P.dma("sp", R.k(*range(16))[:, :, :], U(fm(xT.t[:, t0:t0 + TT])))
            emit_rmsnorm(P, C, R, vec_sb.k(None)[:, 88:104] if False else _cols(vec_sb, 88), hn, TT, tmps)
            P.dma("sp", Acc(fm(ho.t[:, t0:t0 + TT]), ho.buf, (None,)), hn.k(*range(16))[:, :, :], acc=True)
        P.emit({"sp": [ho]})
    return nc


class _cols:
    def __init__(self, tile, base):
        self.tile = tile
        self.base = base

    def __getitem__(self, idx):
        p, c = idx
        return Acc(self.tile.t[p, slice(self.base + c.start, self.base + c.stop)], self.tile.buf, (None,))


def build_p3(last):
    nc = bass.Bass("TRN2", target_bir_lowering=False)
    with contextlib.ExitStack() as st:
        P = Prog(nc, st)
        din = lambda n, s, d: P.dram(n, s, d, kind="ExternalInput", track=False)
        xT = din("xT", [D, TOK], F32)
        hT = din("hT", [D, TOK], BF16)
        hhalo = din("hhalo", [D, 128], BF16)
        yaT = din("yaT", [1024, TOK], BF16)
        ydT = din("ydT", [1024, TOK], BF16)
        w_in = din("w_in", [D, IN_W], F32)
        w_gate = din("w_gate", [D, 4 * D], F32)
        w_br = din("w_br", [4 * 1024, D], F32)
        w_out = din("w_out", [D, D], F32)
        convw = din("convw", [128, 8, 31], F32)
        vecs = din("vecs", [128, NVEC], F32)
        sln = din("sln", [128, 2, 1024], F32)
        sguwT = din("sguwT", [128, 8, 128], F32)
        sgub = din("sgub", [128, 8, 128], F32)
        tril = din("tril", [128, 128], F32)
        if last:
            yo = P.dram("yo", [D, TOK], F32, kind="ExternalOutput")
        else:
            xo = P.dram("xo", [D, TOK], F32, kind="ExternalOutput")
            ho = P.dram("ho", [D, TOK], BF16, kind="ExternalOutput")
        C = Ctx(P)
        vec_sb = P.sb("vec_sb", [128, NVEC], F32)
        cw_sb = P.sb("cw_sb", [128, 8, 31], F32)
        sln_sb = P.sb("sln_sb", [128, 2, 1024], F32)
        sgub_sb = P.sb("sgub_sb", [128, 8, 128], F32)
        wTm = P.sb("wTm", [128, 8, 128], BF16)
        P.dma("sp", vec_sb[:], U(vecs.t[:, :]))
        P.dma("sp", cw_sb[:], U(convw.t[:, :, :]))
        P.dma("sp", sln_sb[:], U(sln.t[:, :, :]))
        P.dma("sp", sgub_sb[:], U(sgub.t[:, :, :]))
        cb = _cols(vec_sb, 0)
        clg = _cols(vec_sb, 8)
        clb = _cols(vec_sb, 16)
        bg = _cols(vec_sb, 24)
        ngc = _cols(vec_sb, 88)
        R = P.sb("R", [128, 16, 544], F32)
        hTt = P.sb("hTt", [128, 16, 544], BF16)
        wbuf = [P.sb(f"wbuf{i}", [128, 16, 512], BF16) for i in range(3)]
        wctr = [0]
        ybT = P.sb("ybT", [128, 8, TT], BF16)
        ycT = P.sb("ycT", [128, 8, TT], BF16)
        yat = P.sb("yat", [128, 8, TT], BF16)
        ydt = P.sb("ydt", [128, 8, TT], BF16)
        mrg = P.sb("mrg", [128, 16, TT], BF16)
        vn = P.sb("vn", [128, 4, 1024], BF16)
        macc = P.sb("macc", [128, 4, TT], F32)
        ft = [P.sb(f"ft{i}", [128, TT], F32) for i in range(6)]
        fctr = [0]
        tmps = {"sq": ft[0:2],
                "hi": [P.sb(f"hi{i}", [128, TT], BF16) for i in range(2)],
                "lo": [P.sb(f"lo{i}", [128, TT], BF16) for i in range(2)],
                "stat": P.sb("stat", [128, TT], F32)}
        mean_sb = P.sb("mean_sb", [128, TT], F32)
        rstd_sb = P.sb("rstd_sb", [128, TT], F32)
        st4 = P.sb("st4", [128, 8], F32)
        hob = [P.sb(f"hob{i}", [128, TT], BF16) for i in range(2)]

        def ftmp():
            t = ft[2 + fctr[0] % 4]
            fctr[0] += 1
            return t

        def wload(src_ap, nk=16):
            t = wbuf[wctr[0] % 3]
            wctr[0] += 1
            P.dma("pool", t[:, 0:nk, :], U(src_ap.rearrange("(kc p) n -> p kc n", p=128)))
            return t

        sg_st = ft[0]
        tril_sb = ft[1]
        P.dma("sp", tril_sb[:, 0:128], U(tril.t[:, :]))
        for g in range(8):
            P.dma("sp", sg_st[:, 0:128], U(sguwT.t[:, g, :]))
            P.tt("dve", wTm.k(g)[:, g, :], sg_st[:, 0:128], tril_sb[:, 0:128], ALU.mult)

        for ti in range(TOK // TT):
            t0 = ti * TT
            P.dma("sp", hTt.k("m")[:, :, 32:544], U(fm(hT.t[:, t0:t0 + TT])))
            if ti == 0:
                P.dma("sp", hTt.k("h")[:, :, 0:32], U(fm(hhalo.t[:, 96:128])))
            else:
                P.dma("sp", hTt.k("h")[:, :, 0:32], U(fm(hT.t[:, t0 - 32:t0])))
            hmain = lambda kc: hTt.k("m")[:, kc, 32:544]

            def proj_fm(wt, jj, nk=16, rhs=None):
                pb_ = C.bank()
                for kc in range(nk):
                    P.mm(pb_[:, :], wt[:, kc, jj * 128:(jj + 1) * 128], hmain(kc) if rhs is None else rhs(kc),
                         start=(kc == 0), stop=(kc == nk - 1))
                return pb_

            for q in range(2):
                wa = wload(w_in.t[:, OFF_BA + 512 * q: OFF_BA + 512 * (q + 1)])
                wb = wload(w_in.t[:, OFF_BB + 512 * q: OFF_BB + 512 * (q + 1)])
                for jj in range(4):
                    j = 4 * q + jj
                    if ti > 0:
                        P.cp("dve", R.k(j)[:, j, 0:32], R.k(j)[:, j, 512:544])
                    pa = proj_fm(wa, jj)
                    pb = proj_fm(wb, jj)
                    sg = ftmp()
                    P.act(sg[:, :], pb[:, :], AF.Sigmoid)
                    P.tt("dve", R.k(j)[:, j, 32:544], pa[:, :], sg[:, :], ALU.mult)
                    if ti == 0:
                        pha = C.bank()
                        phb = C.bank()
                        for kc in range(16):
                            P.mm(pha[:, 0:32], wa[:, kc, jj * 128:(jj + 1) * 128], hTt.k("h")[:, kc, 0:32],
                                 start=(kc == 0), stop=(kc == 15))
                        for kc in range(16):
                            P.mm(phb[:, 0:32], wb[:, kc, jj * 128:(jj + 1) * 128], hTt.k("h")[:, kc, 0:32],
                                 start=(kc == 0), stop=(kc == 15))
                        sg2 = ftmp()
                        P.act(sg2[:, 0:32], phb[:, 0:32], AF.Sigmoid)
                        P.tt("dve", R.k(j)[:, j, 0:32], pha[:, 0:32], sg2[:, 0:32], ALU.mult)
            for j in range(8):
                cj = R.k(8 + j)[:, 8 + j, 0:512]
                P.ts("dve", cj, R.k(j)[:, j, 2:514], cw_sb[:, j, 0:1], cb[:, j:j + 1], ALU.mult, ALU.add)
                for k in range(1, 31):
                    P.stt("dve", cj, R.k(j)[:, j, 2 + k:514 + k], cw_sb[:, j, k:k + 1], cj, ALU.mult, ALU.add)
            pmean = C.bank()
            pmsq = C.bank()
            for j in range(8):
                cj = R.k(8 + j)[:, 8 + j, 0:512]
                sq = tmps["sq"][j % 2]
                P.act(sq[:, :], cj, AF.Square)
                for src, pdst in ((cj, pmean), (sq[:, :], pmsq)):
                    hi = tmps["hi"][j % 2]
                    lo = tmps["lo"][j % 2]
                    P.cp("dve", hi[:, :], src)
                    P.tt("dve", lo[:, :], src, hi[:, :], ALU.subtract)
                    P.mm(pdst[:, :], C.ones_b[:], hi[:, :], start=(j == 0), stop=False)
                    P.mm(pdst[:, :], C.ones_b[:], lo[:, :], start=False, stop=(j == 7))
            P.act(mean_sb[:, :], pmean[:, :], AF.Identity, scale=1.0 / 1024)
            m2 = ftmp()
            P.tt("dve", m2[:, :], mean_sb[:, :], mean_sb[:, :], ALU.mult)
            P.stt("dve", rstd_sb[:, :], pmsq[:, :], 1.0 / 1024, m2[:, :], ALU.mult, ALU.subtract)
            P.act(rstd_sb[:, :], rstd_sb[:, :], AF.Ln, bias=C.eps_col[:, 0:1])
            P.act(rstd_sb[:, :], rstd_sb[:, :], AF.Exp, scale=-0.5)
            for q in range(2):
                wg = wload(w_in.t[:, OFF_BG + 512 * q: OFF_BG + 512 * (q + 1)])
                for jj in range(4):
                    j = 4 * q + jj
                    cj = R.k(8 + j)[:, 8 + j, 0:512]
                    pg = proj_fm(wg, jj)
                    sgt = ftmp()
                    P.act(sgt[:, :], pg[:, :], AF.Silu)
                    t1 = ftmp()
                    P.tt("dve", t1[:, :], cj, mean_sb[:, :], ALU.subtract)
                    P.tt("dve", t1[:, :], t1[:, :], rstd_sb[:, :], ALU.mult)
                    P.act(t1[:, :], t1[:, :], AF.Silu, bias=clb[:, j:j + 1], scale=clg[:, j:j + 1])
                    P.tt("dve", ybT.k(j)[:, j, :], t1[:, :], sgt[:, :], ALU.mult)

            for q in range(2):
                wv = wload(w_in.t[:, OFF_CV + 512 * q: OFF_CV + 512 * (q + 1)])
                for tb in range(4):
                    pv = C.bank()
                    for kc in range(16):
                        P.mm(pv[:, :], hTt.k("m")[:, kc, 32 + tb * 128: 32 + (tb + 1) * 128], wv[:, kc, :],
                             start=(kc == 0), stop=(kc == 15))
                    P.act(R.k(2 * tb + q)[:, 2 * tb + q, 0:512], pv[:, :], AF.Gelu_apprx_tanh)
            for tb in range(4):
                for q in range(2):
                    vq = R.k(2 * tb + q)[:, 2 * tb + q, 0:512]
                    P.op("dve", lambda e, o=st4[:, q:q + 1], i=vq: e.reduce_sum(out=o.ap, in_=i.ap, axis=AX.X),
                         reads=[vq], writes=[st4[:, q:q + 1]])
                    sqv = ftmp()
                    P.tt("dve", sqv[:, :], vq, vq, ALU.mult)
                    P.op("dve", lambda e, o=st4[:, 2 + q:3 + q], i=sqv[:, :]: e.reduce_sum(out=o.ap, in_=i.ap, axis=AX.X),
                         reads=[sqv[:, :]], writes=[st4[:, 2 + q:3 + q]])
                P.tt("dve", st4[:, 4:5], st4[:, 0:1], st4[:, 1:2], ALU.add)
                P.tt("dve", st4[:, 5:6], st4[:, 2:3], st4[:, 3:4], ALU.add)
                P.ts("dve", st4[:, 4:5], st4[:, 4:5], 1.0 / 1024, None, ALU.mult)
                P.tt("dve", st4[:, 6:7], st4[:, 4:5], st4[:, 4:5], ALU.mult)
                P.stt("dve", st4[:, 7:8], st4[:, 5:6], 1.0 / 1024, st4[:, 6:7], ALU.mult, ALU.subtract)
                P.act(st4[:, 7:8], st4[:, 7:8], AF.Ln, bias=C.eps_col[:, 0:1])
                P.act(st4[:, 7:8], st4[:, 7:8], AF.Exp, scale=-0.5)
                for q in range(2):
                    vq = R.k(2 * tb + q)[:, 2 * tb + q, 0:512]
                    P.ts("dve", vq, vq, st4[:, 4:5], st4[:, 7:8], ALU.subtract, ALU.mult)
                    P.tt("dve", vq, vq, sln_sb[:, 0, q * 512:(q + 1) * 512], ALU.mult)
                    P.tt("dve", vn.k(tb)[:, tb, q * 512:(q + 1) * 512], vq, sln_sb[:, 1, q * 512:(q + 1) * 512], ALU.add)
            for g in range(8):
                pz = C.bank()
                for tb in range(4):
                    P.mm(pz[:, tb * 128:(tb + 1) * 128], vn.k(tb)[:, tb, g * 128:(g + 1) * 128], wTm.k(g)[:, g, :],
                         start=True, stop=True)
                for tb in range(4):
                    P.tt("dve", R.k(8 + g)[:, 8 + g, tb * 128:(tb + 1) * 128], pz[:, tb * 128:(tb + 1) * 128],
                         sgub_sb[:, g, :], ALU.add)
            for q in range(2):
                wu = wload(w_in.t[:, OFF_CU + 512 * q: OFF_CU + 512 * (q + 1)])
                wgc = wload(w_in.t[:, OFF_CG + 512 * q: OFF_CG + 512 * (q + 1)])
                for jj in range(4):
                    j = 4 * q + jj
                    pu = proj_fm(wu, jj)
                    pgc = proj_fm(wgc, jj)
                    ut = ftmp()
                    gt = ftmp()
                    P.act(ut[:, :], pu[:, :], AF.Gelu_apprx_tanh)
                    P.act(gt[:, :], pgc[:, :], AF.Silu)
                    P.tt("dve", ut[:, :], ut[:, :], R.k(8 + j)[:, 8 + j, 0:512], ALU.mult)
                    P.tt("dve", ycT.k(j)[:, j, :], ut[:, :], gt[:, :], ALU.mult)

            P.dma("sp", yat[:, :, :], U(fm(yaT.t[:, t0:t0 + TT])))
            P.dma("sp", ydt[:, :, :], U(fm(ydT.t[:, t0:t0 + TT])))
            Y = [lambda kc: yat[:, kc, :], lambda kc: ybT.k(kc)[:, kc, :], lambda kc: ycT.k(kc)[:, kc, :],
                 lambda kc: ydt[:, kc, :]]
            for q in range(4):
                for n in range(4):
                    wg = wload(w_gate.t[:, n * D + 512 * q: n * D + 512 * (q + 1)])
                    wb = wload(w_br.t[n * 1024:(n + 1) * 1024, 512 * q: 512 * (q + 1)], nk=8)
                    for jj in range(4):
                        j = 4 * q + jj
                        py = proj_fm(wb, jj, nk=8, rhs=Y[n])
                        pg = proj_fm(wg, jj)
                        sg = ftmp()
                        P.act(sg[:, :], pg[:, :], AF.Sigmoid, bias=bg[:, n * 16 + j: n * 16 + j + 1])
                        if n == 0:
                            P.tt("dve", macc.k(jj)[:, jj, :], sg[:, :], py[:, :], ALU.mult)
                        else:
                            P.tt("dve", sg[:, :], sg[:, :], py[:, :], ALU.mult)
                            if n < 3:
                                P.tt("dve", macc.k(jj)[:, jj, :], macc.k(jj)[:, jj, :], sg[:, :], ALU.add)
                            else:
                                P.tt("dve", mrg.k(j)[:, j, :], macc.k(jj)[:, jj, :], sg[:, :], ALU.add)

            P.dma("sp", R.k(*range(16))[:, :, 0:512], U(fm(xT.t[:, t0:t0 + TT])))
            for q in range(4):
                wo = wload(w_out.t[:, 512 * q: 512 * (q + 1)])
                for jj in range(4):
                    e = 4 * q + jj
                    po = proj_fm(wo, jj, rhs=lambda kc: mrg.k(kc)[:, kc, :])
                    P.tt("dve", R.k(e)[:, e, 0:512], po[:, :], R.k(e)[:, e, 0:512], ALU.add)
            if not last:
                P.dma("sp", Acc(fm(xo.t[:, t0:t0 + TT]), xo.buf, (None,)), R.k(*range(16))[:, :, 0:512], acc=True)
            ssq = C.bank()
            for c in range(16):
                sq = tmps["sq"][c % 2]
                hi = tmps["hi"][c % 2]
                lo = tmps["lo"][c % 2]
                P.act(sq[:, :], R.k(c)[:, c, 0:512], AF.Square)
                P.cp("dve", hi[:, :], sq[:, :])
                P.tt("dve", lo[:, :], sq[:, :], hi[:, :], ALU.subtract)
                P.mm(ssq[:, :], C.ones_b[:], hi[:, :], start=(c == 0), stop=False)
                P.mm(ssq[:, :], C.ones_b[:], lo[:, :], start=False, stop=(c == 15))
            stat = tmps["stat"]
            P.act(stat[:, :], ssq[:, :], AF.Ln, bias=C.eps_col[:, 0:1], scale=1.0 / D)
            P.act(stat[:, :], stat[:, :], AF.Exp, scale=-0.5)
            for c in range(16):
                if last:
                    P.stt("dve", R.k(c)[:, c, 0:512], R.k(c)[:, c, 0:512], ngc[:, c:c + 1], stat[:, :], ALU.mult, ALU.mult)
                else:
                    hb = hob[c % 2]
                    P.stt("dve", hb[:, :], R.k(c)[:, c, 0:512], ngc[:, c:c + 1], stat[:, :], ALU.mult, ALU.mult)
                    P.dma("sp", Acc(ho.t[c * 128:(c + 1) * 128, t0:t0 + TT], ho.buf, (None,)), hb[:, :], acc=True)
            if last:
                P.dma("sp", Acc(fm(yo.t[:, t0:t0 + TT]), yo.buf, (None,)), R.k(*range(16))[:, :, 0:512], acc=True)
        P.emit({"sp": [yo] if last else [xo, ho]})
        print("p3 ops", P.nops, "sems", P.n_sems, flush=True)
    return nc


def _cols_pm(v, n):
    return np.ascontiguousarray(np.asarray(v, np.float32).reshape(n, 128).T)


TRIL = np.ascontiguousarray(np.triu(np.ones((128, 128), np.float32)))


def p3_consts(l, inp, next_g):
    vec = np.zeros((128, NVEC), np.float32)
    vec[:, 0:8] = _cols_pm(inp["conv_b"][l], 8)
    vec[:, 8:16] = _cols_pm(inp["conv_ln_g"][l], 8)
    vec[:, 16:24] = _cols_pm(inp["conv_ln_b"][l], 8)
    vec[:, 24:88] = _cols_pm(inp["b_gate"][l], 64)
    vec[:, 88:104] = _cols_pm(next_g, 16)
    convw = np.ascontiguousarray(np.asarray(inp["conv_w"][l], np.float32).T.reshape(8, 128, 31).transpose(1, 0, 2))
    sln = np.ascontiguousarray(np.broadcast_to(
        np.stack([inp["sgu_ln_g"][l], inp["sgu_ln_b"][l]], 0)[None], (128, 2, 1024))).astype(np.float32)
    sguwT = np.ascontiguousarray(np.asarray(inp["sgu_w"][l], np.float32).transpose(2, 0, 1))
    sgub = np.ascontiguousarray(np.broadcast_to(np.asarray(inp["sgu_b"][l], np.float32)[None], (128, 8, 128)))
    return {"vecs": vec, "convw": convw, "sln": sln, "sguwT": sguwT, "sgub": sgub, "tril": TRIL,
            "w_in": np.asarray(inp["w_in"][l]), "w_gate": np.asarray(inp["w_gate"][l]),
            "w_br": np.asarray(inp["w_branch"][l]).reshape(4096, D), "w_out": np.asarray(inp["w_out"][l])}


NEG = -30000.0


def p2_consts():
    j = np.arange(128)
    bf = ml_dtypes.bfloat16
    ident = np.eye(128, dtype=np.float32)
    lt = (j[:, None] >= j[None, :]).astype(np.float32)
    inval = (j[:, None] >= j[None, :])
    negm = np.where(inval, NEG, 0.0).astype(np.float32)
    a = np.arange(128)[:, None]
    c = np.arange(256)[None, :]
    band = np.where((c >= a) & (c <= a + 128), 0.0, NEG).astype(np.float32)
    band0 = np.where(np.arange(128)[None, :] <= a, 0.0, NEG).astype(np.float32)
    cb = np.concatenate([ident, lt, negm, -negm], 1).astype(bf)
    cf = np.concatenate([band, band0, ident], 1).astype(np.float32)
    return {"cb": np.ascontiguousarray(cb), "cf": np.ascontiguousarray(cf)}


def emit_dilated(P, C, E):
    BIG, vcur, gd, w_sb, cf, ident = E["BIG"], E["vcur"], E["gd"], E["w_sb"], E["cf"], E["ident"]
    wd, vscr, oscr, ydT, project = E["wd"], E["vscr"], E["oscr"], E["ydT"], E["project"]
    band, band0 = cf[:, 0:256], cf[:, 256:384]
    P.dma("pool", w_sb[:, :, :], U(wd.t[:, :].rearrange("(kc p) n -> p kc n", p=128)))
    project([(b, b, "copy") for b in range(6)], 768, 896)
    P.dma("sp", Acc(vscr.t.rearrange("(b a) d -> a b d", a=128), vscr.buf, (None,)), vcur[:, :, :])
    NW = 2
    sbw = E["sbw"]
    wk = [dict(sm=Sub(sbw[i]["e"], 0, 256, "sm"), st=Sub(sbw[i]["e"], 256, 8, "st"),
               ot=Sub(sbw[i]["s32"], 0, 132, "ot"), p16=Sub(sbw[i]["sp"], 0, 256, "p16"),
               pT=Sub(sbw[i]["w"], 0, 256, "pT")) for i in range(NW)]
    for w in wk:
        P.memset("pool", w["ot"][:, :], 0.0)
    jn = 0
    for gi, d in enumerate((1, 4, 16)):
        nb = 64 // d
        if d > 1:
            P.dma("sp", Acc(vcur.t.rearrange("a (r b) d -> a r b d", r=d), vcur.buf, (None,)),
                  Acc(vscr.t.rearrange("(b a r) d -> a r b d", a=128, r=d), vscr.buf, (None,)))
        oview = oscr[gi].t.rearrange("(b a r) c -> a r b c", a=128, r=d)
        for r in range(d):
            for b in range(nb):
                W = wk[jn % NW]
                jn += 1
                sm, p16, pT, ot, stt_ = W["sm"], W["p16"], W["pT"], W["ot"], W["st"]
                q_lo = d * 128 * b + r
                qs = slice(q_lo, q_lo + d * 127 + 1, d)
                if b == 0:
                    nk, ks, mask, vids = 128, qs, band0, [r * nb]
                else:
                    k_lo = d * 128 * (b - 1) + r
                    nk, ks, mask, vids = 256, slice(k_lo, k_lo + d * 255 + 1, d), band, [r * nb + b - 1, r * nb + b]
                ps = C.bank()
                P.mm(ps[:, 0:nk], BIG[:, gi, qs], BIG[:, 3 + gi, ks], start=True, stop=True)
                P.stt("dve", sm[:, 0:nk], ps[:, 0:nk], SCALE, (cf[:, 0:256] if b else cf[:, 256:384]), ALU.mult, ALU.add)
                P.op("dve", lambda e, o=stt_[:, 0:1], i=sm[:, 0:nk]: e.reduce_max(out=o.ap, in_=i.ap, axis=AX.X),
                     reads=[sm[:, 0:nk]], writes=[stt_[:, 0:1]])
                P.ts("dve", stt_[:, 1:2], stt_[:, 0:1], -1.0, None, ALU.mult)
                P.act(sm[:, 0:nk], sm[:, 0:nk], AF.Exp, bias=stt_[:, 1:2])
                P.op("dve", lambda e, o=stt_[:, 2:3], i=sm[:, 0:nk]: e.reduce_sum(out=o.ap, in_=i.ap, axis=AX.X),
                     reads=[sm[:, 0:nk]], writes=[stt_[:, 2:3]])
                P.cp("dve", p16[:, 0:nk], sm[:, 0:nk])
                half = (jn % 2) * 256
                ptp = C.ptb.k(half)
                for hh in range(nk // 128):
                    P.tr(ptp[:, half + hh * 128: half + (hh + 1) * 128], p16[:, hh * 128:(hh + 1) * 128], ident)
                P.cp("act", pT[:, 0:nk], ptp[:, half:half + nk])
                pso = C.bank()
                for hh in range(nk // 128):
                    P.mm(pso[:, 0:128], pT[:, hh * 128:(hh + 1) * 128], vcur[:, vids[hh], :],
                         start=(hh == 0), stop=(hh == nk // 128 - 1))
                P.op("dve", lambda e, o=stt_[:, 3:4], i=stt_[:, 2:3]: e.reciprocal(out=o.ap, in_=i.ap),
                     reads=[stt_[:, 2:3]], writes=[stt_[:, 3:4]])
                P.ts("dve", ot[:, 0:128], pso[:, 0:128], stt_[:, 3:4], None, ALU.mult)
                P.act(stt_[:, 4:5], stt_[:, 2:3], AF.Ln)
                P.tt("dve", ot[:, 128:129], stt_[:, 4:5], stt_[:, 0:1], ALU.add)
                P.dma("sp", Acc(oview[:, r, b, :], oscr[gi].buf, (None,)), ot[:, :], acc=True)
    cw = [dict(o=[Sub(sbw[i]["e"], 264, 132, "o0"), Sub(sbw[i]["s32"], 132, 132, "o1"), Sub(sbw[i]["s32"], 264, 132, "o2")],
               yb=Sub(sbw[i]["yo"], 0, 128, "yb"), st=Sub(sbw[i]["s32"], 400, 12, "cst")) for i in range(2)]
    for tb in range(64):
        W = cw[tb % 2]
        o, yb, s_ = W["o"], W["yb"], W["st"]
        y = Sub(o[0].tile, o[0].c0, 128, "o0")
        for g in range(3):
            P.dma("sp", o[g][:, :], Acc(oscr[g].t[tb * 128:(tb + 1) * 128, :], oscr[g].buf, (None,)))
        P.tt("dve", s_[:, 0:1], o[0][:, 128:129], o[1][:, 128:129], ALU.max)
        P.tt("dve", s_[:, 0:1], s_[:, 0:1], o[2][:, 128:129], ALU.max)
        P.ts("dve", s_[:, 1:2], s_[:, 0:1], -1.0, None, ALU.mult)
        for g in range(3):
            P.act(s_[:, 2 + g:3 + g], o[g][:, 128:129], AF.Exp, bias=s_[:, 1:2])
        P.tt("dve", s_[:, 5:6], s_[:, 2:3], s_[:, 3:4], ALU.add)
        P.tt("dve", s_[:, 5:6], s_[:, 5:6], s_[:, 4:5], ALU.add)
        P.op("dve", lambda e, o_=s_[:, 6:7], i=s_[:, 5:6]: e.reciprocal(out=o_.ap, in_=i.ap),
             reads=[s_[:, 5:6]], writes=[s_[:, 6:7]])
        P.ts("dve", s_[:, 8:11], s_[:, 2:5], s_[:, 6:7], None, ALU.mult)
        P.ts("dve", y[:, :], o[0][:, 0:128], s_[:, 8:9], None, ALU.mult)
        P.stt("dve", y[:, :], o[1][:, 0:128], s_[:, 9:10], y[:, :], ALU.mult, ALU.add)
        P.stt("dve", y[:, :], o[2][:, 0:128], s_[:, 10:11], y[:, :], ALU.mult, ALU.add)
        P.tt("dve", yb[:, :], y[:, :], gd[:, tb, :], ALU.mult)
        half = (tb % 2) * 256
        ptp = C.ptb.k(half)
        P.tr(ptp[:, half:half + 128], yb[:, :], ident)
        P.cp("act", BIG[:, 0, tb * 128:(tb + 1) * 128], ptp[:, half:half + 128])
    for c4 in range(4):
        P.dma("sp", Acc(ydT.t[:, c4 * 2048:(c4 + 1) * 2048], ydT.buf, (None,)), BIG[:, 0, c4 * 2048:(c4 + 1) * 2048], acc=True)


def build_p2(do_a=True, do_d=True, npairs=8):
    nc = bass.Bass("TRN2", target_bir_lowering=False)
    with contextlib.ExitStack() as st:
        P = Prog(nc, st)
        din = lambda n, s, d: P.dram(n, s, d, kind="ExternalInput", track=False)
        hTf = din("hTf", [D, S], BF16)
        wa = din("wa", [D, 512], F32)
        wd = din("wd", [D, 1024], F32)
        cbd = din("cb", [128, 512], BF16)
        cfd = din("cf", [128, 512], F32)
        yaT = P.dram("yaT", [128, S], BF16, kind="ExternalOutput")
        ydT = P.dram("ydT", [128, S], BF16, kind="ExternalOutput")
        vscr = P.dram("vscr", [S, 128], BF16) if do_d else None
        oscr = [P.dram(f"oscr{g}", [S, 132], F32) for g in range(3)] if do_d else None
        C = Ctx(P)
        cb = P.sb("cb_sb", [128, 512], BF16)
        cf = P.sb("cf_sb", [128, 512], F32)
        P.dma("sp", cb[:], U(cbd.t[:, :]))
        P.dma("sp", cf[:], U(cfd.t[:, :]))
        ident = cb[:, 0:128]
        LT = cb[:, 128:256]
        NEGM = cb[:, 256:384]
        POSM = cb[:, 384:512]
        BIG = P.sb("BIG", [128, 6, S], BF16)
        vcur = P.sb("vcur", [128, 64, 128], BF16)
        gd = P.sb("gd", [128, 64, 128], BF16)
        w_sb = P.sb("w_sb", [128, 16, 1024], BF16)
        hch = [P.sb(f"hch{i}", [128, 16, 256], BF16) for i in range(2)]
        CH = 256

        def project(fm_slots, v_col, g_col):
            for ch in range(S // CH):
                t0 = ch * CH
                hc = hch[ch % 2]
                P.dma("sp", hc[:, :, :], U(fm(hTf.t[:, t0:t0 + CH])))
                for bi, (cblk, slot, kind) in enumerate(fm_slots):
                    pb = C.bank()
                    for kc in range(16):
                        P.mm(pb[:, 0:CH], w_sb[:, kc, cblk * 128:(cblk + 1) * 128], hc[:, kc, :], start=(kc == 0), stop=(kc == 15))
                    dst = BIG.k((slot, ch))[:, slot, t0:t0 + CH]
                    if kind == "copy":
                        P.cp("dve" if bi % 2 == 0 else "act", dst, pb[:, 0:CH])
                    elif kind == "kpair":
                        P.act(dst, pb[:, 0:CH], AF.Identity, scale=SCALE)
                        P.ts("dve", BIG.k((slot + 1, ch))[:, slot + 1, t0:t0 + CH], dst, -1.0, None, ALU.mult)
                    elif kind == "silu":
                        P.act(dst, pb[:, 0:CH], AF.Silu)
                for tb in range(CH // 128):
                    blk = ch * (CH // 128) + tb
                    pv = C.bank()
                    for kc in range(16):
                        P.mm(pv[:, 0:128], hc[:, kc, tb * 128:(tb + 1) * 128], w_sb[:, kc, v_col:v_col + 128],
                             start=(kc == 0), stop=(kc == 15))
                    P.cp("dve", vcur.k(blk)[:, blk, :], pv[:, 0:128])
                    if g_col is not None:
                        pg = C.bank()
                        for kc in range(16):
                            P.mm(pg[:, 0:128], hc[:, kc, tb * 128:(tb + 1) * 128], w_sb[:, kc, g_col:g_col + 128],
                                 start=(kc == 0), stop=(kc == 15))
                        P.act(gd.k(blk)[:, blk, :], pg[:, 0:128], AF.Silu)

        sbw = [dict(e=P.sb(f"sb_e{i}", [128, 512], F32), sp=P.sb(f"sb_sp{i}", [128, 512], BF16),
                    w=P.sb(f"sb_w{i}", [128, 512], BF16), s32=P.sb(f"sb_s32{i}", [128, 512], F32),
                    s16=P.sb(f"sb_s16{i}", [128, 512], BF16), yo=P.sb(f"sb_yo{i}", [128, 512], BF16),
                    A=C.psum[3 * i], B=C.psum[3 * i + 1], O=C.psum[3 * i + 2]) for i in range(2)]
        if do_a:
            P.dma("pool", w_sb[:, :, 0:512], U(wa.t[:, :].rearrange("(kc p) n -> p kc n", p=128)))
            project([(0, 0, "copy"), (1, 1, "kpair"), (3, 3, "silu")], 256, None)
            def sb_chunk(i, W):
                q0 = i * 512
                P.memset("pool", W["s32"][:, :], 0.0)
                P.memset("pool", W["s16"][:, :], 0.0)
                P.memset("pool", W["w"][:, :], 0.0)
                blocks = [(4 * i + m, m) for m in (3, 2, 1, 0)] + [(kb, None) for kb in range(4 * i - 1, -1, -1)]
                for n, (kb, m) in enumerate(blocks):
                    c0 = 0 if m is None else 128 * m
                    ks = slice(kb * 128, (kb + 1) * 128)
                    qs = slice(q0 + c0, q0 + 512)
                    cs = slice(c0, 512)
                    A, B, O = W["A"], W["B"], W["O"]
                    if m is None:
                        P.mm(A[:, :], BIG[:, 1, ks], BIG[:, 0, qs], start=True, stop=True)
                    else:
                        d1 = slice(c0, c0 + 128)
                        P.mm(A[:, d1], BIG[:, 1, ks], BIG[:, 0, q0 + c0:q0 + c0 + 128], start=True, stop=False)
                        P.mm(A[:, d1], ident, NEGM, start=False, stop=True)
                        if m < 3:
                            P.mm(A[:, c0 + 128:512], BIG[:, 1, ks], BIG[:, 0, q0 + c0 + 128:q0 + 512], start=True, stop=True)
                    yield
                    P.act(W["e"][:, cs], A[:, cs], AF.Exp)
                    P.act(W["sp"][:, cs], W["e"][:, cs], AF.Ln, bias=1.0)
                    yield
                    regions = [(cs, False)] if m is None else ([(slice(c0, c0 + 128), True)] + ([(slice(c0 + 128, 512), False)] if m < 3 else []))
                    for (rs, diag) in regions:
                        P.mm(B[:, rs], BIG[:, 2, ks], BIG[:, 0, q0 + rs.start:q0 + rs.stop], start=True, stop=False)
                        if diag:
                            P.mm(B[:, rs], ident, POSM, start=False, stop=False)
                        if n > 0:
                            P.mm(B[:, rs], C.ones_b[:], W["s16"][:, rs], start=False, stop=False)
                        P.mm(B[:, rs], LT, W["sp"][:, rs], start=False, stop=True)
                    P.tt("dve", W["s32"][:, cs], W["s32"][:, cs], W["sp"][:, cs], ALU.add)
                    P.cp("dve", W["s16"][:, cs], W["s32"][:, cs])
                    yield
                    P.act(W["w"][:, cs], B[:, cs], AF.Exp, scale=-1.0)
                    yield
                    last = (n == len(blocks) - 1)
                    P.mm(O[:, :], vcur[:, kb, :], W["w"][:, :], start=(n == 0), stop=last)
                    yield
                P.tt("dve", W["yo"][:, :], W["O"][:, :], BIG[:, 3, q0:q0 + 512], ALU.mult)
                P.dma("sp", Acc(yaT.t[:, q0:q0 + 512], yaT.buf, (None,)), W["yo"][:, :], acc=True)

            for p in range(npairs):
                gens = [sb_chunk(p, sbw[0]), sb_chunk(15 - p, sbw[1])]
                alive = [True, True]
                while any(alive):
                    for gi in range(2):
                        if alive[gi]:
                            try:
                                next(gens[gi])
                            except StopIteration:
                                alive[gi] = False
        if do_d:
            emit_dilated(P, C, locals())
        outs = [t for t in (yaT, ydT) if any(sl[0] == t.buf.name for sl in P.dma_slots)]
        P.emit({"sp": outs})
        print("p2 ops", P.nops, "sems", P.n_sems, flush=True)
    return nc


def p2_weights(l, inp, c):
    w = inp["w_in"][l]
    h = slice(c * 128, (c + 1) * 128)
    wa = np.concatenate([w[:, OFF_AQ:OFF_AQ + 1024][:, h], w[:, OFF_AK:OFF_AK + 1024][:, h],
                         w[:, OFF_AV:OFF_AV + 1024][:, h], w[:, OFF_AG:OFF_AG + 1024][:, h]], 1)
    cols = [w[:, OFF_DQ + g * 1024: OFF_DQ + (g + 1) * 1024][:, h] for g in range(3)]
    cols += [w[:, OFF_DK + g * 1024: OFF_DK + (g + 1) * 1024][:, h] for g in range(3)]
    cols += [w[:, OFF_DV:OFF_DV + 1024][:, h], w[:, OFF_DG:OFF_DG + 1024][:, h]]
    return {"wa": np.ascontiguousarray(wa, dtype=np.float32), "wd": np.ascontiguousarray(np.concatenate(cols, 1), dtype=np.float32)}


_PROGS = {}


def _prog(name):
    if name not in _PROGS:
        if name == "p1":
            _PROGS[name] = build_p1()
        elif name == "p2":
            _PROGS[name] = build_p2()
        elif name == "p3":
            _PROGS[name] = build_p3(False)
        elif name == "p3last":
            _PROGS[name] = build_p3(True)
    return _PROGS[name]


def _run(name, in_maps):
    res = run_bass_kernel_spmd(_prog(name), in_maps, core_ids=list(range(NCORE)))
    return res.results


def kernel(x, norm_g, w_in, conv_w, conv_b, conv_ln_g, conv_ln_b, sgu_ln_g, sgu_ln_b, sgu_w, sgu_b,
           w_branch, w_gate, b_gate, w_out, final_g):
    inp = dict(norm_g=np.asarray(norm_g), w_in=np.asarray(w_in), conv_w=np.asarray(conv_w), conv_b=np.asarray(conv_b),
               conv_ln_g=np.asarray(conv_ln_g), conv_ln_b=np.asarray(conv_ln_b), sgu_ln_g=np.asarray(sgu_ln_g),
               sgu_ln_b=np.asarray(sgu_ln_b), sgu_w=np.asarray(sgu_w), sgu_b=np.asarray(sgu_b),
               w_branch=np.asarray(w_branch), w_gate=np.asarray(w_gate), b_gate=np.asarray(b_gate),
               w_out=np.asarray(w_out))
    x = np.asarray(x, np.float32)[0]
    bf = ml_dtypes.bfloat16
    xT = [np.ascontiguousarray(x[c * TOK:(c + 1) * TOK].T) for c in range(NCORE)]
    vec0 = np.zeros((128, NVEC), np.float32)
    vec0[:, 88:104] = _cols_pm(inp["norm_g"][0], 16)
    r = _run("p1", [{"xT": xT[c], "vecs": vec0} for c in range(NCORE)])
    hT = [r[c]["ho"] for c in range(NCORE)]
    c2 = p2_consts()
    out = None
    for l in range(DEPTH):
        last = (l == DEPTH - 1)
        hTf = np.ascontiguousarray(np.concatenate(hT, axis=1))
        r2 = _run("p2", [dict(c2, hTf=hTf, **p2_weights(l, inp, c)) for c in range(NCORE)])
        yaT = np.concatenate([r2[c]["yaT"] for c in range(NCORE)], axis=0)
        ydT = np.concatenate([r2[c]["ydT"] for c in range(NCORE)], axis=0)
        consts = p3_consts(l, inp, np.asarray(final_g) if last else inp["norm_g"][l + 1])
        ins = []
        for c in range(NCORE):
            sl = slice(c * TOK, (c + 1) * TOK)
            d = dict(consts)
            d["xT"] = xT[c]
            d["hT"] = hT[c]
            d["hhalo"] = np.ascontiguousarray(hT[c - 1][:, TOK - 128:]) if c > 0 else np.zeros((D, 128), bf)
            d["yaT"] = np.ascontiguousarray(yaT[:, sl])
            d["ydT"] = np.ascontiguousarray(ydT[:, sl])
            ins.append(d)
        r3 = _run("p3last" if last else "p3", ins)
        if last:
            out = np.concatenate([r3[c]["yo"].T for c in range(NCORE)], axis=0)
        else:
            xT = [r3[c]["xo"] for c in range(NCORE)]
            hT = [r3[c]["ho"] for c in range(NCORE)]
    return np.ascontiguousarray(out[None].astype(np.float32))
```
